# Optimizing a Trainium2 kernel written in Bass

```python
import math
import jax, jax.numpy as jnp
from jax import lax
import numpy as np

D_MODEL = 1024
BATCH = 1
SEQ = 16384
DEPTH = 1

D_MIX = D_MODEL
RET_WIDTH = D_MIX // 2
RET_HEADS = 8
RET_HEAD_DIM = RET_WIDTH // RET_HEADS
CONV_WIDTH = D_MIX - RET_WIDTH
CONV_GROUPS = 8
CONV_GROUP_DIM = CONV_WIDTH // CONV_GROUPS
SHORT_CONV_K = 3
CHUNK = 128
D_FF = 2816
FFN_CONV_K = 3
ROPE_BASE = 10000.0
EPS = 1e-6
N_MOD = 6
IN_COLS = 4 * RET_WIDTH + 3 * CONV_WIDTH

kernel_name = "hymba_retention_shortconv_convglu_adaln"


def rms_norm(x, g):
    xf = x.astype(jnp.float32)
    y = xf * lax.rsqrt(jnp.mean(xf * xf, axis=-1, keepdims=True) + EPS)
    return (y * g.astype(jnp.float32)).astype(x.dtype)


def group_rms_norm(x, g, n_groups):
    b, s, w = x.shape
    xf = x.astype(jnp.float32).reshape(b, s, n_groups, w // n_groups)
    xf = xf * lax.rsqrt(jnp.mean(xf * xf, axis=-1, keepdims=True) + EPS)
    return (xf.reshape(b, s, w) * g.astype(jnp.float32)).astype(x.dtype)


def modulate(h, shift, scale):
    return h * (1.0 + scale[:, None, :]) + shift[:, None, :]


def causal_dwconv(x, w):
    k = w.shape[0]
    s = x.shape[1]
    xp = jnp.pad(x, ((0, 0), (k - 1, 0), (0, 0)))
    y = xp[:, 0:s, :] * w[0]
    for j in range(1, k):
        y = y + xp[:, j:j + s, :] * w[j]
    return y


def rotary(x, positions):
    half = x.shape[-1] // 2
    inv_freq = ROPE_BASE ** (-jnp.arange(half, dtype=jnp.float32) / half)
    ang = positions.astype(jnp.float32)[..., None] * inv_freq
    cos = jnp.cos(ang)[:, :, None, :]
    sin = jnp.sin(ang)[:, :, None, :]
    xf = x.astype(jnp.float32)
    x1, x2 = xf[..., :half], xf[..., half:]
    return jnp.concatenate([x1 * cos - x2 * sin, x1 * sin + x2 * cos], axis=-1)


def retention_chunkwise(q, k, v):
    b, s, h, dh = q.shape
    nc = s // CHUNK
    log_gamma = jnp.log1p(-jnp.exp2(-5.0 - jnp.arange(h, dtype=jnp.float32)))
    q = q.reshape(b, nc, CHUNK, h, dh)
    k = k.reshape(b, nc, CHUNK, h, dh) * (dh ** -0.5)
    v = v.reshape(b, nc, CHUNK, h, dh)
    idx = jnp.arange(CHUNK, dtype=jnp.float32)
    rel = idx[:, None] - idx[None, :]
    decay_mask = jnp.where(rel[None] >= 0.0,
                           jnp.exp(jnp.maximum(rel, 0.0)[None] * log_gamma[:, None, None]),
                           0.0)
    scores = jnp.einsum('bnqhd,bnkhd->bnhqk', q, k) * decay_mask[None, None]
    y_inner = jnp.einsum('bnhqk,bnkhd->bnqhd', scores, v)
    zeta = jnp.exp((CHUNK - 1.0 - idx)[None, :] * log_gamma[:, None])
    kv_chunk = jnp.einsum('bnkhd,hk,bnkhe->bnhde', k, zeta, v)
    chunk_decay = jnp.exp(CHUNK * log_gamma)[None, :, None, None]

    def step(state, kv):
        return state * chunk_decay + kv, state

    init = jnp.zeros((b, h, dh, dh), dtype=jnp.float32)
    _, r_prev = lax.scan(step, init, jnp.moveaxis(kv_chunk, 1, 0))
    r_prev = jnp.moveaxis(r_prev, 0, 1)
    xi = jnp.exp((idx + 1.0)[None, :] * log_gamma[:, None])
    y_cross = jnp.einsum('bnqhd,bnhde,hq->bnqhe', q, r_prev, xi)
    return (y_inner + y_cross).reshape(b, s, h, dh)


def setup_inputs(seed: int = 0) -> dict:
    key = jax.random.key(seed)
    ks = jax.random.split(key, 20)
    f32 = jnp.float32
    d = D_MODEL

    def nrm(k, shape, scale):
        return jax.random.normal(k, shape, dtype=f32) * scale

    x = nrm(ks[0], (BATCH, SEQ, d), 1.0)
    c = nrm(ks[1], (BATCH, d), 1.0)
    positions = jnp.broadcast_to(jnp.arange(SEQ, dtype=jnp.int32)[None, :], (BATCH, SEQ))
    return {
        "x": x,
        "c": c,
        "positions": positions,
        "mod_w": nrm(ks[2], (DEPTH, d, N_MOD * d), 0.5 * d ** -0.5),
        "mod_b": nrm(ks[3], (DEPTH, N_MOD * d), 0.01),
        "norm1_g": 1.0 + nrm(ks[4], (DEPTH, d), 0.02),
        "w_in": nrm(ks[5], (DEPTH, d, IN_COLS), d ** -0.5),
        "ret_norm_g": 1.0 + nrm(ks[6], (DEPTH, RET_WIDTH), 0.02),
        "short_conv_w": nrm(ks[7], (DEPTH, SHORT_CONV_K, CONV_WIDTH), SHORT_CONV_K ** -0.5),
        "conv_norm_g": 1.0 + nrm(ks[8], (DEPTH, CONV_WIDTH), 0.02),
        "w_out": nrm(ks[9], (DEPTH, D_MIX, d), D_MIX ** -0.5),
        "norm2_g": 1.0 + nrm(ks[10], (DEPTH, d), 0.02),
        "w_up": nrm(ks[11], (DEPTH, d, 2 * D_FF), d ** -0.5),
        "ffn_conv_w": nrm(ks[12], (DEPTH, FFN_CONV_K, D_FF), FFN_CONV_K ** -0.5),
        "ffn_conv_b": nrm(ks[13], (DEPTH, D_FF), 0.01),
        "w_down": nrm(ks[14], (DEPTH, D_FF, d), D_FF ** -0.5),
        "final_mod_w": nrm(ks[15], (d, 2 * d), 0.5 * d ** -0.5),
        "final_mod_b": nrm(ks[16], (2 * d,), 0.01),
        "final_norm_g": 1.0 + nrm(ks[17], (d,), 0.02),
    }


def reference(x, c, positions, mod_w, mod_b, norm1_g, w_in, ret_norm_g, short_conv_w,
              conv_norm_g, w_out, norm2_g, w_up, ffn_conv_w, ffn_conv_b, w_down,
              final_mod_w, final_mod_b, final_norm_g):
    b, s, d = x.shape
    c_act = jax.nn.silu(c)
    for layer in range(DEPTH):
        mod = (c_act @ mod_w[layer] + mod_b[layer]).reshape(b, N_MOD, d)
        shift1, scale1, gate1 = mod[:, 0], mod[:, 1], mod[:, 2]
        shift2, scale2, gate2 = mod[:, 3], mod[:, 4], mod[:, 5]

        h = modulate(rms_norm(x, norm1_g[layer]), shift1, scale1)
        proj = h @ w_in[layer]
        q, k, v, g, cb, cc, cx = jnp.split(
            proj,
            [RET_WIDTH, 2 * RET_WIDTH, 3 * RET_WIDTH, 4 * RET_WIDTH,
             4 * RET_WIDTH + CONV_WIDTH, 4 * RET_WIDTH + 2 * CONV_WIDTH],
            axis=-1)
        q = rotary(q.reshape(b, s, RET_HEADS, RET_HEAD_DIM), positions)
        k = rotary(k.reshape(b, s, RET_HEADS, RET_HEAD_DIM), positions)
        v = v.reshape(b, s, RET_HEADS, RET_HEAD_DIM).astype(jnp.float32)
        y_ret = retention_chunkwise(q, k, v).reshape(b, s, RET_WIDTH).astype(x.dtype)
        y_ret = group_rms_norm(y_ret, ret_norm_g[layer], RET_HEADS) * jax.nn.silu(g)
        y_conv = cb * causal_dwconv(cc * cx, short_conv_w[layer])
        y_conv = group_rms_norm(y_conv, conv_norm_g[layer], CONV_GROUPS)
        mixed = jnp.concatenate([y_ret, y_conv], axis=-1) @ w_out[layer]
        x = x + gate1[:, None, :] * mixed

        h = modulate(rms_norm(x, norm2_g[layer]), shift2, scale2)
        up = h @ w_up[layer]
        a, val = up[..., :D_FF], up[..., D_FF:]
        a = causal_dwconv(a, ffn_conv_w[layer]) + ffn_conv_b[layer]
        ffn = (jax.nn.silu(a) * val) @ w_down[layer]
        x = x + gate2[:, None, :] * ffn

    fmod = (c_act @ final_mod_w + final_mod_b).reshape(b, 2, d)
    return modulate(rms_norm(x, final_norm_g), fmod[:, 0], fmod[:, 1])
```

```python
import math
import os
import numpy as np
from contextlib import ExitStack
import concourse.bass as bass
import concourse.mybir as mybir
from concourse.bass_utils import run_bass_kernel_spmd

F32 = mybir.dt.float32
BF16 = mybir.dt.bfloat16
I32 = mybir.dt.int32
AF = mybir.ActivationFunctionType
ALU = mybir.AluOpType
AX = mybir.AxisListType

NCORES = 8
D = 1024
T = 2048
NCH = 16
DFF = 2816
NFC = 22
EPS = 1e-6
PI = math.pi

ENGS = ("pe", "act", "dve", "pool", "sp")
NDSEM = 8
STAGE = int(os.environ.get("KSTAGE", "9"))
KCUT = int(os.environ.get("KCUT", "0"))
KCUT2 = int(os.environ.get("KCUT2", "0"))


def cut2(n):
    if KCUT2 == n:
        raise _Stop()


class _Stop(Exception):
    pass


class Prog:
    def __init__(self):
        self.ops = {e: [] for e in ENGS}
        self.cnt = {e: 0 for e in ENGS}
        self.seen = {e: {} for e in ENGS}
        self.last_w = {}
        self.readers = {}
        self.dma_k = {e: 0 for e in ENGS}
        self.extra_sems = []
        self.ranges = {}

    def _expand(self, names):
        out = []
        for n in names:
            out.append(n)
            if n in self.ranges:
                b, lo, hi = self.ranges[n]
                for m, (b2, lo2, hi2) in self.ranges.items():
                    if m != n and b2 == b and lo2 < hi and lo < hi2:
                        out.append(m)
        return out

    def _deps(self, reads, writes):
        deps = {}

        def add(k, v):
            if deps.get(k, 0) < v:
                deps[k] = v

        for r in self._expand(reads):
            if r in self.last_w:
                t = self.last_w[r]
                add(t[:2], t[2])
        for w in self._expand(writes):
            if w in self.last_w:
                t = self.last_w[w]
                add(t[:2], t[2])
            for k, v in self.readers.get(w, {}).items():
                add(k, v)
        return deps

    def _finish(self, eng, emit, deps, token, inc, reads, writes):
        waits = []
        for k, v in deps.items():
            if k[0] == "E" and k[1] == "pe" and eng == "pe":
                continue
            if self.seen[eng].get(k, 0) >= v:
                continue
            self.seen[eng][k] = v
            waits.append((k, v))
        self.ops[eng].append((waits, emit, inc))
        for r in reads:
            d = self.readers.setdefault(r, {})
            k = token[:2]
            if d.get(k, 0) < token[2]:
                d[k] = token[2]
        for w in writes:
            self.last_w[w] = token
            self.readers[w] = {}

    @staticmethod
    def _excl(reads, writes):
        def isbank(n):
            return len(n) >= 2 and n[0] == "B" and n[1].isdigit()
        w = list(writes)
        for r in reads:
            if isbank(r) and r not in w:
                w.append(r)
        return list(reads), w

    def op(self, eng, emit, reads=(), writes=()):
        reads, writes = self._excl(reads, writes)
        deps = self._deps(reads, writes)
        self.cnt[eng] += 1
        token = ("E", eng, self.cnt[eng])
        self._finish(eng, emit, deps, token, (("E", eng), 1), reads, writes)

    def dma(self, eng, emit, reads=(), writes=()):
        deps = self._deps(reads, writes)
        k = self.dma_k[eng]
        self.dma_k[eng] += 1
        si = k % NDSEM
        val = 16 * (k // NDSEM + 1)
        if val > 16:
            kk = ("D", (eng, si))
            if deps.get(kk, 0) < val - 16:
                deps[kk] = val - 16
        token = ("D", (eng, si), val)
        self._finish(eng, emit, deps, token, (("D", (eng, si)), 16), reads, writes)

    def coll(self, emit, name, reads=(), writes=()):
        deps = self._deps(reads, writes)
        self.extra_sems.append(name)
        token = ("C", name, 1)
        self._finish("pool", emit, deps, token, (("C", name), 1), reads, writes)

    def wait_all(self, eng, resources):
        deps = self._deps(resources, ())
        waits = []
        for k, v in deps.items():
            if self.seen[eng].get(k, 0) >= v:
                continue
            self.seen[eng][k] = v
            waits.append((k, v))
        self.ops[eng].append((waits, None, None))

    def drain(self, eng="sp"):
        waits = []
        for e in ENGS:
            if self.cnt[e]:
                waits.append((("E", e), self.cnt[e]))
            for si in range(NDSEM):
                n = (self.dma_k[e] - si + NDSEM - 1) // NDSEM
                if n > 0:
                    waits.append((("D", (e, si)), 16 * n))
        for n in self.extra_sems:
            waits.append((("C", n), 1))
        self.ops[eng].append((waits, None, None))

    def emit(self, nc, stack):
        sems = {}
        for e in ENGS:
            sems[("E", e)] = stack.enter_context(nc.semaphore("es_" + e))
            for i in range(NDSEM):
                if self.dma_k[e] > i:
                    sems[("D", (e, i))] = stack.enter_context(nc.semaphore("ds_%s%d" % (e, i)))
        for n in self.extra_sems:
            sems[("C", n)] = stack.enter_context(nc.semaphore("cs_" + n))
        block = stack.enter_context(nc.Block())
        ops = self.ops

        def run(engname):
            def f(eng):
                for waits, emit, inc in ops[engname]:
                    for k, v in waits:
                        eng.wait_ge(sems[k], v)
                    if emit is None:
                        continue
                    ins = emit(eng)
                    if inc is not None:
                        if inc[0][0] == "C":
                            ins.then_inc(sems[inc[0]])
                        else:
                            ins.then_inc(sems[inc[0]], inc[1])
            return f

        block.tensor(run("pe"))
        block.scalar(run("act"))
        block.vector(run("dve"))
        block.gpsimd(run("pool"))
        block.sync(run("sp"))


def build(debug=False):
    nc = bass.Bass("TRN2", target_bir_lowering=False, dynamic_dma_scratch_size=2048)
    P = Prog()

    def din(name, shape, dt=F32):
        return nc.dram_tensor(name, list(shape), dt, kind="ExternalInput").ap()

    x_d = din("x", [T, D]); xh_d = din("xh", [2, D]); pos_d = din("pos", [128, NCH], I32)
    ccol_d = din("ccol", [128, 8]); modw_d = din("modw", [D, 1024]); modb_d = din("modb", [1, 1024])
    grow_d = din("grow", [16, 128]); gf_d = din("gf", [1, D]); retg_d = din("retg", [1, 512])
    cng_d = din("cng", [128, 4]); scw_d = din("scw", [128, 12]); fcw_d = din("fcw", [128, 66]); fcb_d = din("fcb", [128, NFC])
    w_in_d = din("w_in", [D, 3584]); w_out_d = din("w_out", [D, D]); w_up_d = din("w_up", [D, 2 * DFF]); w_down_d = din("w_down", [DFF, D])
    identf_d = din("identf", [128, 128]); maskT_d = din("maskT", [128, 1024]); xi_d = din("xi", [64, 1024])
    zeta_d = din("zeta", [128, 512]); cd_d = din("cd", [64, 512]); wts_d = din("wts", [64, 64]); bones_d = din("bones", [128, 128])
    invf_d = din("invf", [128, 32]); flag_d = din("flag", [128, 1]); sel_d = din("sel", [16, 2])
    out_d = nc.dram_tensor("out", [T, D], F32, kind="ExternalOutput").ap()
    modl_d = nc.dram_tensor("modl", [1, 1024], F32).ap(); moda_d = nc.dram_tensor("moda", [8, 1024], F32).ap()
    Ll_d = nc.dram_tensor("Ll", [64, 512], F32).ap(); La_d = nc.dram_tensor("La", [512, 512], F32).ap()
    xhl_d = nc.dram_tensor("xhl", [2, D], F32).ap(); xha_d = nc.dram_tensor("xha", [16, D], F32).ap()
    if debug:
        x1_d = nc.dram_tensor("x1s", [T, D], F32, kind="ExternalOutput").ap()
        dbgA = nc.dram_tensor("dbgA", [128, 2048], F32, kind="ExternalOutput").ap()
    else:
        x1_d = nc.dram_tensor("x1s", [T, D], F32).ap()

    with ExitStack() as st:
        arena = st.enter_context(nc.sbuf_tensor("arena", [128, 33792], F32))
        SCRN = 20480
        scr = st.enter_context(nc.sbuf_tensor("scr", [128, SCRN], F32))
        banks = [st.enter_context(nc.psum_tensor("B%d" % i, [128, 512], F32)) for i in range(8)]
        stg = [st.enter_context(nc.sbuf_tensor("stg%d" % i, [128, 1024], F32)) for i in range(2)]
        lc_n = [0]

        def carve(base, bname, name, off_b, shape, dt):
            esz = 4 if dt in (F32, I32) else 2
            n = int(np.prod(shape[1:]))
            nb = n * esz
            assert off_b % 4 == 0 and nb % 4 == 0
            ap = base[0:shape[0], off_b // 4:(off_b + nb) // 4]
            if dt != F32:
                ap = ap.bitcast(dt)
            if len(shape) == 3:
                ap = ap.rearrange("p (a b) -> p a b", a=shape[1])
            elif len(shape) == 4:
                ap = ap.rearrange("p (a b c) -> p a b c", a=shape[1], b=shape[2])
            P.ranges[name] = (bname, off_b, off_b + nb)
            return ap

        class Bump:
            def __init__(self, base, bname, start, limit):
                self.base, self.bname, self.off, self.limit = base, bname, start, limit

            def __call__(self, name, shape, dt):
                esz = 4 if dt in (F32, I32) else 2
                nb = int(np.prod(shape[1:])) * esz
                nb4 = (nb + 3) // 4 * 4
                ap = carve(self.base, self.bname, name, self.off, shape, dt)
                self.off += nb4
                assert self.off <= self.limit, (name, self.off, self.limit)
                return ap

        G = Bump(scr, "scr", 0, SCRN * 4)
        identb = G("identb", [128, 128], BF16); identf = G("identf", [128, 128], F32)
        bones = G("bones", [128, 128], BF16)
        mcols = G("mcols", [128, 80], F32); a1 = G("a1", [128, 8], F32); a2 = G("a2", [128, 8], F32)
        cng = G("cng", [128, 4], F32); scw = G("scw", [128, 4, 3], F32); fcw = G("fcw", [128, NFC, 3], F32); fcb = G("fcb", [128, NFC], F32)
        wts = G("wts", [64, 64], F32); flag = G("flag", [128, 1], F32); sel = G("sel", [16, 2], F32)
        ssq = G("ssq", [128, 8], F32); rt = G("rt", [128, 8], F32); rstd = G("rstd", [128, 8], F32)
        ysum = G("ysum", [128, 8], F32); yrs = G("yrs", [128, 8], F32)
        ccol = G("ccol", [128, 8], F32); cact = G("cact", [128, 8], F32)
        rows_sb = G("rows_sb", [80, 128], F32)
        R = G("R", [64, 512], F32); Rbf = G("Rbf", [64, 512], BF16); Lst = G("Lst", [64, 512], F32); Stmp = G("Stmp", [64, 512], F32)
        halo = G("halo", [128, NFC, 2], F32)
        xhs = G("xhs", [2, D], F32); xhn = G("xhn", [2, D], BF16); hTh = G("hTh", [128, 8, 2], BF16)
        gbase = G.off

        A = Bump(scr, "scr", gbase, SCRN * 4)
        junk = A("junk", [128, D], BF16)
        xn = [A("xn%d" % i, [128, D], BF16) for i in range(2)]
        qkrot = A("qkrot", [128, 1024], BF16); vbf = A("vbf", [128, 512], BF16); vz = A("vz", [128, 512], BF16)
        sg = A("sg", [128, 512], F32)
        qT = A("qT", [64, 8, 128], BF16); qxiT = A("qxiT", [64, 8, 128], BF16); kT = A("kT", [64, 8, 128], BF16)
        sT = A("sT", [128, 8, 128], BF16)
        ynb = A("ynb", [128, 512], BF16); ynT = A("ynT", [128, 4, 128], BF16)
        cxs = A("cxs", [128, 512], F32); ct = A("ct", [128, 512], F32); ycf = A("ycf", [128, 512], F32)
        ysq = cxs; yn = ct
        sqb = A("sqb", [128, 512], BF16); rs = A("rs", [128, 512], F32); ycT = A("ycT", [128, 4, 512], BF16)
        ssb = [ycf, rs]
        otmp = [A("otmp%d" % i, [128, 512], F32) for i in range(2)]
        maskT = A("maskT", [128, 8, 128], F32); xi = A("xi", [64, 8, 128], F32); zeta = A("zeta", [128, 512], F32); cd = A("cd", [64, 512], F32)
        cos_t = A("cos_t", [128, NCH, 32], F32); sin_t = A("sin_t", [128, NCH, 32], F32); nsin_t = A("nsin_t", [128, NCH, 32], F32)
        gate1 = A("gate1", [128, D], F32); retg = A("retg", [128, 512], F32)
        posi = A("posi", [128, NCH], I32)
        w_in = carve(arena, "arena", "w_in", 0, [128, 8, 3584], BF16)
        w_out = carve(arena, "arena", "w_out", 57344, [128, 8, 1024], BF16)
        modw = carve(arena, "arena", "modw", 0, [128, 8, 1024], F32)
        H = Bump(arena, "arena", 73728, 33792 * 4)
        xt = [H("xt%d" % i, [128, D], F32) for i in range(4)]
        hT = H("hT", [128, 8, 512], BF16)
        u = H("u", [128, 4, 514], F32)
        Lg = carve(arena, "arena", "Lg", H.off, [64, 8, 512], F32)
        tmpA = [H("tmpA%d" % i, [128, 512], F32) for i in range(2)]
        tmpB = [H("tmpB%d" % i, [128, 512], F32) for i in range(2)]
        x1t = [H("x1t%d" % i, [128, D], F32) for i in range(2)]
        modb = carve(arena, "arena", "modb", P.ranges["x1t0"][1], [1, 1024], F32)
        modrow = carve(arena, "arena", "modrow", P.ranges["x1t1"][1], [1, 1024], F32)
        Bm = Bump(scr, "scr", gbase, SCRN * 4)
        junkB = Bm("junkB", [128, D], BF16)
        xnB = [Bm("xnB%d" % i, [128, D], BF16) for i in range(2)]
        x1in = [Bm("x1in%d" % i, [128, D], F32) for i in range(2)]
        h2T = Bm("h2T", [128, 8, 256], BF16)
        abuf = [Bm("abuf%d" % i, [128, 258], F32) for i in range(2)]
        ctB = [Bm("ctB%d" % i, [128, 256], F32) for i in range(2)]
        sB = [Bm("sB%d" % i, [128, 256], F32) for i in range(2)]
        vB = [Bm("vB%d" % i, [128, 256], F32) for i in range(2)]
        actT = Bm("actT", [128, NFC, 256], BF16)
        otmpB = [Bm("otmpB%d" % i, [128, 512], F32) for i in range(2)]
        x2 = [Bm("x2_%d" % i, [128, D], F32) for i in range(2)]
        gate2 = Bm("gate2", [128, D], F32); afb = Bm("afb", [128, D], F32); fsh = Bm("fsh", [128, D], F32)
        gfb = carve(scr, "scr", "gfb", P.ranges["x2_1"][1], [128, D], F32)
        xhs16 = carve(scr, "scr", "xhs16", P.ranges["x2_0"][1], [16, D], F32)
        w_down = carve(arena, "arena", "w_down", 0, [128, NFC, 1024], BF16)
        w_up = carve(arena, "arena", "w_up", 45056, [128, 8, 2 * DFF], BF16)

        Bf = [b[:] for b in banks]
        B4h = banks[4][:].bitcast(BF16)
        B7h = banks[7][:].bitcast(BF16).rearrange("p (a b) -> p a b", a=8)
        B7f = banks[7][:].bitcast(BF16)


        def sub(name, parent, lo, hi):
            b, plo, phi = P.ranges[parent]
            P.ranges[name] = (b, plo + lo, plo + hi)
        for nm, par in (("modw_b", "modw"), ("modrow0", "modrow"), ("modrow1", "modrow"), ("rows_a", "rows_sb"), ("rows_b", "rows_sb"),
                        ("tmpB0_b", "tmpB0"), ("tmpB1_b", "tmpB1"),
                        ("u_h", "u"), ("sT0", "sT"), ("sT1", "sT"), ("abuf0_h", "abuf0"), ("abuf1_h", "abuf1")):
            P.ranges[nm] = P.ranges[par]
        for s_ in range(2):
            for n_ in range(2):
                P.ranges["x1t%d_%d" % (s_, n_)] = P.ranges["x1t%d" % s_]
                P.ranges["x2_%d_%d" % (s_, n_)] = P.ranges["x2_%d" % s_]
        for k_ in range(1, 8):
            sub("w_in_%d" % k_, "w_in", k_ * 7168, (k_ + 1) * 7168)
        P.ranges["w_in"] = ("arena", 0, 7168)
        for c_ in range(8):
            for nm in ("ssq", "rt", "rstd"):
                P.ranges["%s%d" % (nm, c_)] = (nm + "x", c_, c_ + 1)
        for cc_ in range(4):
            for k_ in range(8):
                sub("hT_%d_%d" % (cc_, k_), "hT", (k_ * 512 + cc_ * 128) * 2, (k_ * 512 + cc_ * 128 + 128) * 2)
        del P.ranges["hT"]
        sub("qkrot_q", "qkrot", 0, 1024); sub("qkrot_k", "qkrot", 1024, 2048)
        del P.ranges["qkrot"]
        for k_ in range(8):
            for hf_ in range(2):
                for q_ in range(4):
                    c0_ = k_ * 5632 + hf_ * 2816 + q_ * 704
                    sub("w_up_%d_%d_%d" % (k_, hf_, q_), "w_up", c0_ * 2, (c0_ + 704) * 2)
        del P.ranges["w_up"]
        for f0_ in range(0, NFC, 4):
            sub("w_down_%d" % f0_, "w_down", f0_ * 2048, min(NFC, f0_ + 4) * 2048)
        del P.ranges["w_down"]
        for tc_ in range(2):
            for k_ in range(8):
                sub("h2T_%d_%d" % (tc_, k_), "h2T", (k_ * 256 + tc_ * 128) * 2, (k_ * 256 + tc_ * 128 + 128) * 2)
        del P.ranges["h2T"]
        for fc_ in range(NFC):
            sub("actT_%d" % fc_, "actT", fc_ * 512, (fc_ + 1) * 512)
            sub("halo_%d" % fc_, "halo", fc_ * 8, (fc_ + 1) * 8)
        del P.ranges["actT"]

        def DMA(q, out, in_, reads, writes, **kw):
            P.dma(q, lambda e: e.dma_start(out=out, in_=in_, **kw), reads=reads, writes=writes)

        def ACT(out, in_, func, reads, writes, scale=1.0, bias=None, accum=None):
            def f(e):
                kw = {}
                if bias is not None:
                    kw["bias"] = bias
                if accum is not None:
                    kw["accum_out"] = accum
                return e.activation(out=out, in_=in_, func=func, scale=scale, **kw)
            P.op("act", f, reads=reads, writes=writes)

        def TT(eng, out, in0, in1, op, reads, writes):
            P.op(eng, lambda e: e.tensor_tensor(out=out, in0=in0, in1=in1, op=op), reads=reads, writes=writes)

        def TS(eng, out, in0, s1, s2, op0, op1, reads, writes):
            if op1 is None:
                P.op(eng, lambda e: e.tensor_scalar(out=out, in0=in0, scalar1=s1, scalar2=None, op0=op0), reads=reads, writes=writes)
            else:
                P.op(eng, lambda e: e.tensor_scalar(out=out, in0=in0, scalar1=s1, scalar2=s2, op0=op0, op1=op1), reads=reads, writes=writes)

        def STT(out, in0, s, in1, op0, op1, reads, writes):
            P.op("dve", lambda e: e.scalar_tensor_tensor(out=out, in0=in0, scalar=s, in1=in1, op0=op0, op1=op1), reads=reads, writes=writes)

        def CP(eng, out, in_, reads, writes):
            P.op(eng, lambda e: e.tensor_copy(out=out, in_=in_), reads=reads, writes=writes)

        def MMG(mms, reads, writes):
            def f(e):
                ins = None
                for (o, l, r, s0, s1) in mms:
                    ins = e.matmul(o, lhsT=l, rhs=r, start=s0, stop=s1)
                return ins
            P.op("pe", f, reads=reads, writes=writes)

        def TRG(trs, reads, writes):
            def f(e):
                ins = None
                for (o, i, idn) in trs:
                    ins = e.transpose(out=o, in_=i, identity=idn)
                return ins
            P.op("pe", f, reads=reads, writes=writes)


        def load_cast(dst, src, n, res):
            i = lc_n[0]; lc_n[0] += 1
            sl = i % 2
            DMA("sp", stg[sl][:, 0:n], src, [], ["stg%d" % sl])
            if i % 2 == 0:
                ACT(dst, stg[sl][:, 0:n], AF.Copy, ["stg%d" % sl], [res])
            else:
                CP("dve", dst, stg[sl][:, 0:n], ["stg%d" % sl], [res])

        try:
            DMA("sp", ccol, ccol_d, [], ["ccol"])
            DMA("sp", modb, modb_d, [], ["modb"])
            mw_v = modw_d.rearrange("(k p) n -> p k n", p=128)
            for hlf in range(2):
                DMA("sp", modw[:, hlf * 4:(hlf + 1) * 4, :], mw_v[:, hlf * 4:(hlf + 1) * 4, :], [], ["modw"] if hlf == 0 else ["modw_b"])
            DMA("sp", identf, identf_d, [], ["identf"])
            CP("dve", identb, identf, ["identf"], ["identb"])
            load_cast(bones, bones_d, 128, "bones")
            for (ap, d_, nm) in ((cng, cng_d, "cng"), (scw.rearrange("p a b -> p (a b)"), scw_d, "scw"), (fcw.rearrange("p a b -> p (a b)"), fcw_d, "fcw"),
                                 (fcb, fcb_d, "fcb"), (wts, wts_d, "wts"), (flag, flag_d, "flag"), (sel, sel_d, "sel"),
                                 (posi, pos_d, "posi"), (maskT.rearrange("p a b -> p (a b)"), maskT_d, "maskT"), (xi.rearrange("p a b -> p (a b)"), xi_d, "xi"),
                                 (zeta, zeta_d, "zeta"), (cd, cd_d, "cd"), (xhs, xh_d, "xhs")):
                DMA("sp", ap, d_, [], [nm])
            DMA("sp", retg, retg_d.partition_broadcast(128), [], ["retg"])
            ACT(cact, ccol, AF.Silu, ["ccol"], ["cact"])
            for n in range(2):
                MMG([(Bf[n][0:1, :], cact[:, k:k + 1], modw[:, k, n * 512:(n + 1) * 512], k == 0, k == 7) for k in range(8)],
                    ["cact", "modw", "modw_b"], ["B%d" % n])
                TT("dve", modrow[:, n * 512:(n + 1) * 512], Bf[n][0:1, :], modb[:, n * 512:(n + 1) * 512], ALU.add, ["B%d" % n, "modb"], ["modrow%d" % n])
            DMA("sp", modl_d, modrow, ["modrow0", "modrow1"], ["modl_d"])
            P.coll(lambda e: e.collective_compute("AllGather", ALU.bypass, replica_groups=[list(range(NCORES))],
                                                  ins=[modl_d.opt()], outs=[moda_d.opt()]), "ag_mod", reads=["modl_d"], writes=["moda_d"])
            DMA("sp", rows_sb[0:64, :], moda_d.rearrange("v (k p) -> (v k) p", p=128), ["moda_d"], ["rows_a"])
            DMA("sp", rows_sb[64:80, :], grow_d, [], ["rows_b"])
            DMA("sp", gate1, moda_d[2:3, :].partition_broadcast(128), ["moda_d"], ["gate1"])
            wi_v = w_in_d.rearrange("(k p) n -> p k n", p=128)
            for k in range(8):
                for q4 in range(4):
                    load_cast(w_in[:, k, q4 * 896:(q4 + 1) * 896], wi_v[:, k, q4 * 896:(q4 + 1) * 896], 896, "w_in" if k == 0 else "w_in_%d" % k)
            WIN = ["w_in"] + ["w_in_%d" % k for k in range(1, 8)]
            wo_v = w_out_d.rearrange("(k p) n -> p k n", p=128)
            for k in range(8):
                load_cast(w_out[:, k, :], wo_v[:, k, :], 1024, "w_out")
            TRG([(Bf[2][:, 0:80], rows_sb[0:80, :], identf[0:80, 0:80])], ["rows_a", "rows_b", "identf"], ["B2"])
            CP("dve", mcols, Bf[2][:, 0:80], ["B2"], ["mcols"])
            STT(a1, mcols[:, 8:16], 1.0, mcols[:, 64:72], ALU.add, ALU.mult, ["mcols"], ["a1"])
            STT(a2, mcols[:, 32:40], 1.0, mcols[:, 72:80], ALU.add, ALU.mult, ["mcols"], ["a2"])
            sh1 = mcols[:, 0:8]; sh2 = mcols[:, 24:32]

            posf = vz.bitcast(F32)[:, 0:NCH]
            CP("dve", posf, posi, ["posi"], ["vz"])
            invf = vbf.bitcast(F32)[:, 0:32]
            DMA("sp", invf, invf_d, [], ["vbf"])
            ang = ct.rearrange("p (a b) -> p a b", a=NCH)
            TT("dve", ang, posf.unsqueeze(2).to_broadcast([128, NCH, 32]), invf.unsqueeze(1).to_broadcast([128, NCH, 32]), ALU.mult, ["vz", "vbf"], ["ct"])
            angf = ct; kq = ycf; ki = cxs.bitcast(I32); kf = rs; rr = sg; mm_ = otmp[0]; rc = otmp[1]
            TS("dve", kq, angf, 1.0 / (2 * PI), None, ALU.mult, None, ["ct"], ["ycf"])
            CP("dve", ki, kq, ["ycf"], ["cxs"])
            CP("dve", kf, ki, ["cxs"], ["rs"])
            C1 = 6.28125; C2 = 2 * PI - C1
            STT(rr, kf, -C1, angf, ALU.mult, ALU.add, ["rs", "ct"], ["sg"])
            STT(rr, kf, -C2, rr, ALU.mult, ALU.add, ["rs", "sg"], ["sg"])
            TS("dve", mm_, rr, PI, -2 * PI, ALU.is_gt, ALU.mult, ["sg"], ["otmp0"])
            TT("dve", rr, rr, mm_, ALU.add, ["sg", "otmp0"], ["sg"])
            TS("dve", mm_, rr, -PI, 2 * PI, ALU.is_lt, ALU.mult, ["sg"], ["otmp0"])
            TT("dve", rr, rr, mm_, ALU.add, ["sg", "otmp0"], ["sg"])
            TS("dve", rc, rr, PI / 2, None, ALU.add, None, ["sg"], ["otmp1"])
            TS("dve", mm_, rc, PI, -2 * PI, ALU.is_gt, ALU.mult, ["otmp1"], ["otmp0"])
            TT("dve", rc, rc, mm_, ALU.add, ["otmp1", "otmp0"], ["otmp1"])
            TS("dve", rr, rr, -PI, PI, ALU.max, ALU.min, ["sg"], ["sg"])
            TS("dve", rc, rc, -PI, PI, ALU.max, ALU.min, ["otmp1"], ["otmp1"])
            cosf = cos_t.rearrange("p a b -> p (a b)"); sinf = sin_t.rearrange("p a b -> p (a b)"); nsinf = nsin_t.rearrange("p a b -> p (a b)")
            ACT(sinf, rr, AF.Sin, ["sg"], ["sin_t"])
            ACT(nsinf, rr, AF.Sin, ["sg"], ["nsin_t"], scale=-1.0)
            ACT(cosf, rc, AF.Sin, ["otmp1"], ["cos_t"])

            def normT(xsrc, xres, n, col, junk_ap, junk_res, xn_ap, xn_res, acol, shcol, colres, dst, dst_res):
                ACT(junk_ap[0:n, :], xsrc, AF.Square, [xres], [junk_res, "ssq%d" % col], accum=ssq[0:n, col:col + 1])
                ACT(rt[0:n, col:col + 1], ssq[0:n, col:col + 1], AF.Sqrt, ["ssq%d" % col], ["rt%d" % col], scale=1.0 / D, bias=EPS)
                P.op("dve", lambda e: e.reciprocal(out=rstd[0:n, col:col + 1], in_=rt[0:n, col:col + 1]), reads=["rt%d" % col], writes=["rstd%d" % col])
                ACT(xn_ap[0:n, :], xsrc, AF.Copy, [xres, "rstd%d" % col], [xn_res], scale=rstd[0:n, col:col + 1])
                TRG([(B7h[:, k, 0:n], xn_ap[0:n, k * 128:(k + 1) * 128], identb[0:n, 0:n]) for k in range(8)], [xn_res, "identb"], ["B7"])
                for k in range(8):
                    if col % 2 == 0:
                        ACT(dst(k), B7h[:, k, 0:n], AF.Identity, ["B7"] + colres, [dst_res(k)], scale=acol[:, k:k + 1], bias=shcol[:, k:k + 1])
                    else:
                        TS("dve", dst(k), B7h[:, k, 0:n], acol[:, k:k + 1], shcol[:, k:k + 1], ALU.mult, ALU.add, ["B7"] + colres, [dst_res(k)])

            def hT_dst(cc):
                return (lambda k: hT[:, k, cc * 128:(cc + 1) * 128]), (lambda k: "hT_%d_%d" % (cc, k))
            HT = lambda cc: ["hT_%d_%d" % (cc, k) for k in range(8)]

            def load_x(c):
                DMA("sp", xt[c % 4], x_d[c * 128:(c + 1) * 128, :], [], ["xt%d" % (c % 4)])

            def rotary(src_bank, bres, c, dst, dres, s):
                sv = src_bank.rearrange("p (h t f) -> p h t f", h=8, t=2)
                cb_ = cos_t[:, c, :].unsqueeze(1).unsqueeze(1).to_broadcast([128, 8, 2, 32])
                sb_ = sin_t[:, c, :].unsqueeze(1).to_broadcast([128, 8, 32])
                nb_ = nsin_t[:, c, :].unsqueeze(1).to_broadcast([128, 8, 32])
                tA = tmpA[s].rearrange("p (h t f) -> p h t f", h=8, t=2); tB = tmpB[s].rearrange("p (h t f) -> p h t f", h=8, t=2)
                TT("dve", tA, sv, cb_, ALU.mult, [bres, "cos_t"], ["tmpA%d" % s])
                TT("dve", tB[:, :, 0, :], sv[:, :, 1, :], nb_, ALU.mult, [bres, "nsin_t"], ["tmpB%d" % s])
                TT("dve", tB[:, :, 1, :], sv[:, :, 0, :], sb_, ALU.mult, [bres, "sin_t"], ["tmpB%d_b" % s])
                TT("dve", dst, tmpA[s], tmpB[s], ALU.add, ["tmpA%d" % s, "tmpB%d" % s, "tmpB%d_b" % s], [dres])

            def kv_update(c, St, Sres, first):
                MMG([(Bf[3][0:64, h * 64:(h + 1) * 64], qkrot[:, 512 + h * 64:512 + (h + 1) * 64], vz[:, h * 64:(h + 1) * 64], True, True) for h in range(8)],
                    ["qkrot_k", "vz"], ["B3"])
                if first:
                    CP("dve", St, Bf[3][0:64, :], ["B3"], [Sres])
                else:
                    TT("dve", St, St, cd, ALU.mult, [Sres, "cd"], [Sres])
                    TT("dve", St, St, Bf[3][0:64, :], ALU.add, ["B3", Sres], [Sres])

            if STAGE < 2:
                raise _Stop()
            load_x(0); load_x(1)
            for c in range(int(os.environ.get("KP1CH", NCH))):
                cc = c % 4
                if c + 2 < NCH:
                    load_x(c + 2)
                d_, r_ = hT_dst(cc)
                normT(xt[c % 4], "xt%d" % (c % 4), 128, c % 8, junk, "junk", xn[c % 2], "xn%d" % (c % 2), a1, sh1, ["a1", "mcols"], d_, r_)
                if KCUT == 1:
                    raise _Stop()
                for n, bk in ((1, 1), (2, 2)):
                    MMG([(Bf[bk], hT[:, k, cc * 128:(cc + 1) * 128], w_in[:, k, n * 512:(n + 1) * 512], k == 0, k == 7) for k in range(8)],
                        HT(cc) + WIN, ["B%d" % bk])
                if os.environ.get("KVZ") == "first":
                    TT("dve", vz, Bf[2], zeta, ALU.mult, ["B2", "zeta"], ["vz"])
                if os.environ.get("KNOROT"):
                    CP("dve", qkrot[:, 512:1024], Bf[1], ["B1"], ["qkrot_k"])
                else:
                    rotary(Bf[1], "B1", c, qkrot[:, 512:1024], "qkrot_k", c % 2)
                if KCUT == 2:
                    raise _Stop()
                if os.environ.get("KVZ") == "first":
                    pass
                elif os.environ.get("KVZ") == "copy":
                    CP("dve", vz, Bf[2], ["B2"], ["vz"])
                elif os.environ.get("KVZ") == "act":
                    ACT(vz, Bf[2], AF.Copy, ["B2"], ["vz"])
                elif os.environ.get("KVZ") == "sb":
                    ACT(vbf, Bf[2], AF.Copy, ["B2"], ["vbf"])
                    TT("dve", vz, vbf, zeta, ALU.mult, ["vbf", "zeta"], ["vz"])
                else:
                    ACT(vbf, Bf[2], AF.Copy, ["B2"], ["vbf"])
                    TT("dve", vz, vbf, zeta, ALU.mult, ["vbf", "zeta"], ["vz"])
                if KCUT == 3:
                    raise _Stop()
                kv_update(c, Lst, "Lst", c == 0)
            DMA("sp", Ll_d, Lst, ["Lst"], ["Ll_d"])
            if not os.environ.get("KNOAG"):
                P.coll(lambda e: e.collective_compute("AllGather", ALU.bypass, replica_groups=[list(range(NCORES))],
                                                      ins=[Ll_d.opt()], outs=[La_d.opt()]), "ag_L", reads=["Ll_d"], writes=["La_d"])
            DMA("sp", Lg, La_d.rearrange("(r p) n -> p r n", p=64), ["La_d"], ["Lg"])
            P.op("dve", lambda e: e.memset(R, 0.0), reads=[], writes=["R"])
            for jc in range(NCORES - 1):
                TT("dve", Stmp.rearrange("p (h e) -> p h e", h=8), Lg[:, jc, :].rearrange("p (h e) -> p h e", h=8),
                   wts[:, jc * 8:(jc + 1) * 8].unsqueeze(2).to_broadcast([64, 8, 64]), ALU.mult, ["Lg", "wts"], ["Stmp"])
                TT("dve", R, R, Stmp, ALU.add, ["R", "Stmp"], ["R"])
            CP("dve", Rbf, R, ["R"], ["Rbf"])
            if debug:
                DMA("sp", dbgA[0:64, 0:512], Lst, ["Lst"], ["dbgA0"])
                DMA("sp", dbgA[64:128, 0:512], R, ["R"], ["dbgA1"])
                DMA("sp", dbgA[:, 512:1024], cosf, ["cos_t"], ["dbgA2"])
                DMA("sp", dbgA[:, 1024:1536], sinf, ["sin_t"], ["dbgA3"])
                DMA("sp", dbgA[:, 1536:1616], mcols, ["mcols"], ["dbgA4"])

            if STAGE < 3:
                raise _Stop()
            normT(xhs, "xhs", 2, 0, junk, "junk", xhn, "xhn", a1, sh1, ["a1", "mcols"], lambda k: hTh[:, k, :], lambda k: "hTh")
            for j in range(4):
                for (bk, col0) in ((1, 2560), (2, 3072)):
                    MMG([(Bf[bk][:, j * 2:(j + 1) * 2], w_in[:, k, col0 + j * 128:col0 + (j + 1) * 128], hTh[:, k, :], k == 0, k == 7) for k in range(8)],
                        ["hTh"] + WIN, ["B%d" % bk])
            ACT(cxs[:, 0:8], Bf[2][:, 0:8], AF.Copy, ["B2"], ["cxs"])
            TT("dve", ct[:, 0:8], Bf[1][:, 0:8], cxs[:, 0:8], ALU.mult, ["B1", "cxs"], ["ct"])
            TS("dve", u[:, :, 0:2], ct[:, 0:8].rearrange("p (a b) -> p a b", a=4), flag[:, 0:1], None, ALU.mult, None, ["ct", "flag"], ["u_h"])
            cut2(1)
            for c in range(4):
                load_x(c)
            for g in range(4):
                for cc in range(4):
                    c = g * 4 + cc
                    d_, r_ = hT_dst(cc)
                    normT(xt[cc], "xt%d" % cc, 128, c % 8, junk, "junk", xn[c % 2], "xn%d" % (c % 2), a1, sh1, ["a1", "mcols"], d_, r_)
                HTall = HT(0) + HT(1) + HT(2) + HT(3)
                cut2(2)
                for j in range(4):
                    for (bk, col0) in ((0, 2048), (1, 2560), (2, 3072)):
                        MMG([(Bf[bk], w_in[:, k, col0 + j * 128:col0 + (j + 1) * 128], hT[:, k, :], k == 0, k == 7) for k in range(8)],
                            HTall + WIN, ["B%d" % bk])
                    ACT(cxs, Bf[2], AF.Copy, ["B2"], ["cxs"])
                    TT("dve", u[:, j, 2:514], Bf[1], cxs, ALU.mult, ["B1", "cxs"], ["u"])
                    TS("dve", ct, u[:, j, 2:514], scw[:, j, 2:3], None, ALU.mult, None, ["u", "u_h", "scw"], ["ct"])
                    STT(ct, u[:, j, 1:513], scw[:, j, 1:2], ct, ALU.mult, ALU.add, ["u", "u_h", "scw", "ct"], ["ct"])
                    STT(ct, u[:, j, 0:512], scw[:, j, 0:1], ct, ALU.mult, ALU.add, ["u", "u_h", "scw", "ct"], ["ct"])
                    TT("dve", ycf, ct, Bf[0], ALU.mult, ["ct", "B0"], ["ycf"])
                    ACT(sqb, ycf, AF.Square, ["ycf"], ["sqb"])
                    MMG([(Bf[3], bones, sqb, True, True)], ["bones", "sqb"], ["B3"])
                    ACT(rs, Bf[3], AF.Sqrt, ["B3"], ["rs"], bias=EPS)
                    P.op("dve", lambda e: e.reciprocal(out=rs, in_=rs), reads=["rs"], writes=["rs"])
                    STT(ycT[:, j, :], ycf, cng[:, j:j + 1], rs, ALU.mult, ALU.mult, ["ycf", "cng", "rs"], ["ycT"])
                    CP("dve", u[:, j, 0:2], u[:, j, 512:514], ["u", "u_h"], ["u_h"])
                cut2(3)
                for cc in range(4):
                    c = g * 4 + cc
                    for n in range(4):
                        MMG([(Bf[n], hT[:, k, cc * 128:(cc + 1) * 128], w_in[:, k, n * 512:(n + 1) * 512], k == 0, k == 7) for k in range(8)],
                            HT(cc) + WIN, ["B%d" % n])
                    rotary(Bf[0], "B0", c, qkrot[:, 0:512], "qkrot_q", 0)
                    rotary(Bf[1], "B1", c, qkrot[:, 512:1024], "qkrot_k", 1)
                    ACT(vbf, Bf[2], AF.Copy, ["B2"], ["vbf"])
                    TT("dve", vz, vbf, zeta, ALU.mult, ["vbf", "zeta"], ["vz"])
                    ACT(sg, Bf[3], AF.Silu, ["B3"], ["sg"])
                    TT("dve", sg, sg, retg, ALU.mult, ["sg", "retg"], ["sg"])
                    cut2(4)
                    TRG([(B4h[0:64, h * 128:(h + 1) * 128], qkrot[:, h * 64:(h + 1) * 64], identb) for h in range(8)], ["qkrot_q", "identb"], ["B4"])
                    TRG([(B7f[0:64, h * 128:(h + 1) * 128], qkrot[:, 512 + h * 64:512 + (h + 1) * 64], identb) for h in range(8)], ["qkrot_k", "identb"], ["B7"])
                    ACT(qT.rearrange("p a b -> p (a b)"), B4h[0:64, :], AF.Copy, ["B4"], ["qT"])
                    ACT(kT.rearrange("p a b -> p (a b)"), B7f[0:64, :], AF.Copy, ["B7"], ["kT"])
                    TT("dve", qxiT.rearrange("p a b -> p (a b)"), qT.rearrange("p a b -> p (a b)"), xi.rearrange("p a b -> p (a b)"), ALU.mult, ["qT", "xi"], ["qxiT"])
                    cut2(5)
                    for half in range(2):
                        mms = []
                        for hl in range(4):
                            h = half * 4 + hl
                            mms.append((Bf[half][:, hl * 128:(hl + 1) * 128], kT[:, h, :], qT[:, h, :], True, True))
                        MMG(mms, ["kT", "qT"], ["B%d" % half])
                        ACT(ssb[half], Bf[half], AF.Copy, ["B%d" % half], [("ycf", "rs")[half]])
                        TT("dve", sT[:, half * 4:(half + 1) * 4, :].rearrange("p a b -> p (a b)"), ssb[half],
                           maskT[:, half * 4:(half + 1) * 4, :].rearrange("p a b -> p (a b)"), ALU.mult, [("ycf", "rs")[half], "maskT"], ["sT%d" % half])
                    mms = []
                    for h in range(8):
                        mms.append((Bf[2][:, h * 64:(h + 1) * 64], sT[:, h, :], vbf[:, h * 64:(h + 1) * 64], True, False))
                        mms.append((Bf[2][:, h * 64:(h + 1) * 64], qxiT[:, h, :], Rbf[:, h * 64:(h + 1) * 64], False, True))
                    MMG(mms, ["sT0", "sT1", "vbf", "qxiT", "Rbf"], ["B2"])
                    cut2(7)
                    kv_update(c, R, "R", False)
                    CP("dve", Rbf, R, ["R"], ["Rbf"])
                    cut2(8)
                    ACT(ysq, Bf[2], AF.Square, ["B2"], ["cxs"])
                    P.op("dve", lambda e: e.tensor_reduce(out=ysum, in_=ysq.rearrange("p (h e) -> p h e", h=8), axis=AX.X, op=ALU.add), reads=["cxs"], writes=["ysum"])
                    ACT(yrs, ysum, AF.Sqrt, ["ysum"], ["yrs"], scale=1.0 / 64, bias=EPS)
                    P.op("dve", lambda e: e.reciprocal(out=yrs, in_=yrs), reads=["yrs"], writes=["yrs"])
                    TT("dve", yn.rearrange("p (h e) -> p h e", h=8), Bf[2].rearrange("p (h e) -> p h e", h=8), yrs.unsqueeze(2).to_broadcast([128, 8, 64]), ALU.mult, ["B2", "yrs"], ["ct"])
                    TT("dve", ynb, yn, sg, ALU.mult, ["ct", "sg"], ["ynb"])
                    cut2(9)
                    TRG([(B4h[:, j * 128:(j + 1) * 128], ynb[:, j * 128:(j + 1) * 128], identb) for j in range(4)], ["ynb", "identb"], ["B4"])
                    ACT(ynT.rearrange("p a b -> p (a b)"), B4h[:, 0:512], AF.Copy, ["B4"], ["ynT"])
                    cut2(10)
                    for n in range(2):
                        mms = [(Bf[5 + n], ynT[:, j, :], w_out[:, j, n * 512:(n + 1) * 512], j == 0, False) for j in range(4)]
                        mms += [(Bf[5 + n], ycT[:, j, cc * 128:(cc + 1) * 128], w_out[:, 4 + j, n * 512:(n + 1) * 512], False, j == 3) for j in range(4)]
                        MMG(mms, ["ynT", "ycT", "w_out"], ["B%d" % (5 + n)])
                        TT("dve", otmp[n], Bf[5 + n], gate1[:, n * 512:(n + 1) * 512], ALU.mult, ["B%d" % (5 + n), "gate1"], ["otmp%d" % n])
                        TT("dve", x1t[c % 2][:, n * 512:(n + 1) * 512], otmp[n], xt[cc][:, n * 512:(n + 1) * 512], ALU.add,
                           ["otmp%d" % n, "xt%d" % cc], ["x1t%d_%d" % (c % 2, n)])
                    DMA("sp", x1_d[c * 128:(c + 1) * 128, :], x1t[c % 2], ["x1t%d_0" % (c % 2), "x1t%d_1" % (c % 2)], ["x1_d"])
                    if c == NCH - 1:
                        DMA("sp", xhl_d, x1t[c % 2][126:128, :], ["x1t%d_0" % (c % 2), "x1t%d_1" % (c % 2)], ["xhl_d"])
                    if c + 4 < NCH:
                        load_x(c + 4)

            if STAGE < 4:
                raise _Stop()
            P.coll(lambda e: e.collective_compute("AllGather", ALU.bypass, replica_groups=[list(range(NCORES))],
                                                  ins=[xhl_d.opt()], outs=[xha_d.opt()]), "ag_xh", reads=["xhl_d"], writes=["xha_d"])
            wu_v = w_up_d.rearrange("(k p) n -> p k n", p=128)
            WUP = []
            for hf in range(2):
                for q2 in range(4):
                    for k in (3, 4, 5, 6, 7, 0, 1, 2):
                        nm = "w_up_%d_%d_%d" % (k, hf, q2)
                        c0 = hf * 2816 + q2 * 704
                        load_cast(w_up[:, k, c0:c0 + 704], wu_v[:, k, c0:c0 + 704], 704, nm)
                        WUP.append(nm)

            def wup_names(hf, fc):
                qs = sorted({(fc * 128) // 704, (fc * 128 + 127) // 704})
                return ["w_up_%d_%d_%d" % (k, hf, q) for q in qs for k in range(8)]
            wd_v = w_down_d.rearrange("(f p) n -> p f n", p=128)
            WDN = []
            for f0 in range(0, NFC, 4):
                f1 = min(NFC, f0 + 4)
                nm = "w_down_%d" % f0
                for f_ in range(f0, f1):
                    load_cast(w_down[:, f_, :], wd_v[:, f_, :], 1024, nm)
                WDN.append(nm)
            DMA("sp", gate2, moda_d[5:6, :].partition_broadcast(128), ["moda_d"], ["gate2"])
            DMA("sp", fsh, moda_d[6:7, :].partition_broadcast(128), ["moda_d"], ["fsh"])
            DMA("sp", afb, moda_d[7:8, :].partition_broadcast(128), ["moda_d"], ["afb"])
            DMA("sp", gfb, gf_d.partition_broadcast(128), [], ["gfb"])
            STT(afb, afb, 1.0, gfb, ALU.add, ALU.mult, ["afb", "gfb"], ["afb"])
            DMA("sp", xhs16, xha_d, ["xha_d"], ["xhs16"])
            for n in range(2):
                MMG([(Bf[5 + n][0:2, :], sel, xhs16[:, n * 512:(n + 1) * 512], True, True)], ["sel", "xhs16"], ["B%d" % (5 + n)])
                CP("dve", xhs[:, n * 512:(n + 1) * 512], Bf[5 + n][0:2, :], ["B%d" % (5 + n)], ["xhs"])
            normT(xhs, "xhs", 2, 0, junkB, "junkB", xhn, "xhn", a2, sh2, ["a2", "mcols"], lambda k: hTh[:, k, :], lambda k: "hTh")
            for fc in range(NFC):
                MMG([(Bf[0][:, fc * 2:(fc + 1) * 2], w_up[:, k, fc * 128:(fc + 1) * 128], hTh[:, k, :], k == 0, k == 7) for k in range(8)],
                    ["hTh"] + wup_names(0, fc), ["B0"])
            TS("dve", halo.rearrange("p a b -> p (a b)"), Bf[0][:, 0:2 * NFC], flag[:, 0:1], None, ALU.mult, None, ["B0", "flag"], ["halo"])

            def load_x1(c):
                DMA("sp", x1in[c % 2], x1_d[c * 128:(c + 1) * 128, :], ["x1_d"], ["x1in%d" % (c % 2)])
            H2 = lambda tc: ["h2T_%d_%d" % (tc, k) for k in range(8)]
            AT = ["actT_%d" % fc for fc in range(NFC)]
            load_x1(0); load_x1(1)
            korder = (3, 4, 5, 6, 7, 0, 1, 2)
            for t8 in range(8):
                for tc in range(2):
                    c = t8 * 2 + tc
                    normT(x1in[tc], "x1in%d" % tc, 128, c % 8, junkB, "junkB", xnB[tc], "xnB%d" % tc, a2, sh2, ["a2", "mcols"],
                          (lambda k, tc=tc: h2T[:, k, tc * 128:(tc + 1) * 128]), (lambda k, tc=tc: "h2T_%d_%d" % (tc, k)))
                for fc in range(NFC):
                    s = fc % 2
                    ba, bv = (0, 1) if s == 0 else (2, 3)
                    MMG([(Bf[ba][:, 0:256], w_up[:, k, fc * 128:(fc + 1) * 128], h2T[:, k, :], i == 0, i == 7) for i, k in enumerate(korder)],
                        H2(0) + H2(1) + wup_names(0, fc), ["B%d" % ba])
                    MMG([(Bf[bv][:, 0:256], w_up[:, k, DFF + fc * 128:DFF + (fc + 1) * 128], h2T[:, k, :], i == 0, i == 7) for i, k in enumerate(korder)],
                        H2(0) + H2(1) + wup_names(1, fc), ["B%d" % bv])
                    ACT(abuf[s][:, 2:258], Bf[ba][:, 0:256], AF.Copy, ["B%d" % ba], ["abuf%d" % s])
                    CP("dve", abuf[s][:, 0:2], halo[:, fc, :], ["halo_%d" % fc, "halo"], ["abuf%d_h" % s])
                    TS("dve", ctB[s], abuf[s][:, 2:258], fcw[:, fc, 2:3], None, ALU.mult, None, ["abuf%d" % s, "fcw"], ["ctB%d" % s])
                    STT(ctB[s], abuf[s][:, 1:257], fcw[:, fc, 1:2], ctB[s], ALU.mult, ALU.add, ["abuf%d" % s, "abuf%d_h" % s, "fcw", "ctB%d" % s], ["ctB%d" % s])
                    STT(ctB[s], abuf[s][:, 0:256], fcw[:, fc, 0:1], ctB[s], ALU.mult, ALU.add, ["abuf%d" % s, "abuf%d_h" % s, "fcw", "ctB%d" % s], ["ctB%d" % s])
                    CP("dve", halo[:, fc, :], abuf[s][:, 256:258], ["abuf%d" % s], ["halo_%d" % fc])
                    ACT(sB[s], ctB[s], AF.Silu, ["ctB%d" % s, "fcb"], ["sB%d" % s], bias=fcb[:, fc:fc + 1])
                    ACT(vB[s], Bf[bv][:, 0:256], AF.Copy, ["B%d" % bv], ["vB%d" % s])
                    TT("dve", actT[:, fc, :], sB[s], vB[s], ALU.mult, ["sB%d" % s, "vB%d" % s], ["actT_%d" % fc])
                for tc in range(2):
                    c = t8 * 2 + tc
                    for n in range(2):
                        bk = 4 + (tc * 2 + n) % 3
                        MMG([(Bf[bk], actT[:, fc, tc * 128:(tc + 1) * 128], w_down[:, fc, n * 512:(n + 1) * 512], fc == 0, fc == NFC - 1) for fc in range(NFC)],
                            AT + WDN, ["B%d" % bk])
                        TT("dve", otmpB[n], Bf[bk], gate2[:, n * 512:(n + 1) * 512], ALU.mult, ["B%d" % bk, "gate2"], ["otmpB%d" % n])
                        TT("dve", x2[tc][:, n * 512:(n + 1) * 512], otmpB[n], x1in[tc][:, n * 512:(n + 1) * 512], ALU.add,
                           ["otmpB%d" % n, "x1in%d" % tc], ["x2_%d_%d" % (tc, n)])
                    X2 = ["x2_%d_0" % tc, "x2_%d_1" % tc]
                    col = c % 8
                    ACT(junkB, x2[tc], AF.Square, X2, ["junkB", "ssq%d" % col], accum=ssq[:, col:col + 1])
                    ACT(rt[:, col:col + 1], ssq[:, col:col + 1], AF.Sqrt, ["ssq%d" % col], ["rt%d" % col], scale=1.0 / D, bias=EPS)
                    P.op("dve", lambda e, col=col: e.reciprocal(out=rstd[:, col:col + 1], in_=rt[:, col:col + 1]), reads=["rt%d" % col], writes=["rstd%d" % col])
                    STT(x2[tc], x2[tc], rstd[:, col:col + 1], afb, ALU.mult, ALU.mult, X2 + ["rstd%d" % col, "afb"], X2)
                    TT("dve", x2[tc], x2[tc], fsh, ALU.add, X2 + ["fsh"], X2)
                    DMA("sp", out_d[c * 128:(c + 1) * 128, :], x2[tc], X2, ["out_d"])
                    if c + 2 < NCH:
                        load_x1(c + 2)

        except _Stop:
            pass
        fin = ["out_d"] + (WIN + ["w_out"] if STAGE < 2 else []) + (["dbgA0", "dbgA1", "dbgA2", "dbgA3", "dbgA4", "x1_d"] if debug else [])
        P.wait_all("sp", [r for r in fin if r in P.last_w])
        P.drain("sp")
        P.emit(nc, st)
    return nc


def _consts(core):
    h = np.arange(8, dtype=np.float64)
    lg = np.log1p(-np.exp2(-5.0 - h))
    idx = np.arange(128, dtype=np.float64)
    rel = idx[None, :] - idx[:, None]
    maskT = np.where(rel[:, None, :] >= 0, np.exp(np.maximum(rel, 0)[:, None, :] * lg[None, :, None]), 0.0) * 0.125
    p = np.arange(128)
    xi = np.broadcast_to(np.exp((idx[None, :] + 1.0) * lg[:, None])[None], (64, 8, 128))
    zeta = np.repeat(np.exp((127.0 - idx)[:, None] * lg[None, :]) * 0.125, 64, axis=1)
    cd = np.broadcast_to(np.repeat(np.exp(128.0 * lg), 64)[None, :], (64, 512))
    wts = np.zeros((64, 8, 8))
    for jc in range(8):
        if jc < core:
            wts[:, jc, :] = np.exp(2048.0 * (core - 1 - jc) * lg)[None, :]
    bones = (p[:, None] // 64 == p[None, :] // 64).astype(np.float64) / 64.0
    half = 32
    invf = (np.float32(10000.0) ** (-np.arange(half, dtype=np.float32) / np.float32(half))).astype(np.float32)
    sel = np.zeros((16, 2), np.float32)
    if core > 0:
        sel[2 * (core - 1), 0] = 1.0
        sel[2 * (core - 1) + 1, 1] = 1.0
    f = lambda a: np.ascontiguousarray(a, dtype=np.float32)
    return dict(identf=f(np.eye(128)), maskT=f(maskT.reshape(128, 1024)), xi=f(xi.reshape(64, 1024)), zeta=f(zeta),
                cd=f(cd), wts=f(wts.reshape(64, 64)), bones=f(bones), invf=f(np.broadcast_to(invf, (128, 32))),
                flag=f(np.full((128, 1), 0.0 if core == 0 else 1.0)), sel=sel)


def make_in_maps(x, c, positions, mod_w, mod_b, norm1_g, w_in, ret_norm_g, short_conv_w, conv_norm_g, w_out, norm2_g,
                 w_up, ffn_conv_w, ffn_conv_b, w_down, final_mod_w, final_mod_b, final_norm_g):
    f = lambda a: np.ascontiguousarray(np.asarray(a), dtype=np.float32)
    x = f(x)[0]; c = f(c)[0]; pos = np.asarray(positions)[0].astype(np.int32)
    modw_all = np.concatenate([f(mod_w)[0], f(final_mod_w)], axis=1)
    modb_all = np.concatenate([f(mod_b)[0], f(final_mod_b)], axis=0)
    grow = np.concatenate([f(norm1_g)[0].reshape(8, 128), f(norm2_g)[0].reshape(8, 128)], axis=0)
    shared = dict(
        ccol=f(c.reshape(8, 128).T), grow=f(grow), gf=f(final_norm_g).reshape(1, D), retg=f(ret_norm_g)[0].reshape(1, 512),
        cng=f(f(conv_norm_g)[0].reshape(4, 128).T),
        scw=f(f(short_conv_w)[0].reshape(3, 4, 128).transpose(2, 1, 0).reshape(128, 12)),
        fcw=f(f(ffn_conv_w)[0].reshape(3, NFC, 128).transpose(2, 1, 0).reshape(128, 3 * NFC)),
        fcb=f(f(ffn_conv_b)[0].reshape(NFC, 128).T),
        w_in=f(w_in)[0], w_out=f(w_out)[0], w_up=f(w_up)[0], w_down=f(w_down)[0])
    maps = []
    for i in range(NCORES):
        m = dict(shared)
        m["x"] = x[i * T:(i + 1) * T]
        m["xh"] = x[i * T - 2:i * T] if i > 0 else np.zeros((2, D), np.float32)
        m["pos"] = np.ascontiguousarray(pos[i * T:(i + 1) * T].reshape(NCH, 128).T)
        m["modw"] = np.ascontiguousarray(modw_all[:, i * 1024:(i + 1) * 1024])
        m["modb"] = np.ascontiguousarray(modb_all[i * 1024:(i + 1) * 1024].reshape(1, 1024))
        m.update(_consts(i))
        maps.append(m)
    return maps


_NC_CACHE = {}


def kernel(**inputs):
    debug = bool(os.environ.get("KDEBUG"))
    if debug not in _NC_CACHE:
        _NC_CACHE[debug] = build(debug)
    nc = _NC_CACHE[debug]
    maps = make_in_maps(**inputs)
    res = run_bass_kernel_spmd(nc, maps, core_ids=list(range(NCORES)))
    out = np.concatenate([r["out"] for r in res.results], axis=0)[None]
    if debug:
        kernel.dbg = res.results
    return out.astype(np.float32)
```
